# Optimizing a Trainium2 kernel written in Bass

```python
import jax, jax.numpy as jnp
from jax import lax
import numpy as np

D_MODEL = 2048
BATCH = 4
SEQ = 4096
DEPTH = 2

N_MIXERS = 2
N_A = (DEPTH + 1) // 2
N_B = DEPTH // 2
CHUNK = 128
A_WIDTH = D_MODEL
A_GROUPS = 16
A_GROUP_DIM = A_WIDTH // A_GROUPS
B_HEADS = 16
B_HEAD_DIM = D_MODEL // B_HEADS
Q_BLOCK = 128
FFN_HIDDEN = ((-(-8 * D_MODEL // 3) + 255) // 256) * 256
N_MOD = 6
EPS = 1e-6

kernel_name = "hybrid_sgu_fox_adaln_trunk"


def rms_norm(x, g):
    xf = x.astype(jnp.float32)
    y = xf * lax.rsqrt(jnp.mean(xf * xf, axis=-1, keepdims=True) + EPS)
    return (y * g.astype(jnp.float32)).astype(x.dtype)


def layer_norm(x, g, b):
    xf = x.astype(jnp.float32)
    mu = jnp.mean(xf, axis=-1, keepdims=True)
    var = jnp.mean(jnp.square(xf - mu), axis=-1, keepdims=True)
    y = (xf - mu) * lax.rsqrt(var + EPS)
    return (y * g.astype(jnp.float32) + b.astype(jnp.float32)).astype(x.dtype)


def modulate(h, shift, scale):
    return h * (1 + scale[:, None, :]) + shift[:, None, :]


def chunk_sgu_mixer(h, w_in, b_in, ln_g, ln_b, w_s, b_s, w_out):
    B, S, _ = h.shape
    z = jax.nn.gelu(h @ w_in + b_in, approximate=False)
    u, v = jnp.split(z, 2, axis=-1)
    v = layer_norm(v, ln_g, ln_b)
    causal = jnp.tril(jnp.ones((CHUNK, CHUNK), dtype=bool))
    w = jnp.where(causal[None], w_s, 0).astype(v.dtype)
    vc = v.reshape(B, S // CHUNK, CHUNK, A_GROUPS, A_GROUP_DIM)
    sv = jnp.einsum('gts,bnsgc->bntgc', w, vc) + b_s.T[None, None, :, :, None].astype(v.dtype)
    y = u * sv.reshape(B, S, A_WIDTH)
    return y @ w_out


def forgetting_attention(h, w_in, b_f, w_out):
    B, S, D = h.shape
    proj = h @ w_in
    q, k, v, f_logit = jnp.split(proj, [D, 2 * D, 3 * D], axis=-1)
    q = q.reshape(B, S, B_HEADS, B_HEAD_DIM).transpose(0, 2, 1, 3)
    k = k.reshape(B, S, B_HEADS, B_HEAD_DIM).transpose(0, 2, 1, 3)
    v = v.reshape(B, S, B_HEADS, B_HEAD_DIM).transpose(0, 2, 1, 3)
    log_f = jax.nn.log_sigmoid(f_logit.astype(jnp.float32) + b_f.astype(jnp.float32))
    F = jnp.cumsum(log_f, axis=1).transpose(0, 2, 1)
    n_blk = S // Q_BLOCK
    q_blocks = q.reshape(B, B_HEADS, n_blk, Q_BLOCK, B_HEAD_DIM).transpose(2, 0, 1, 3, 4)
    F_blocks = F.reshape(B, B_HEADS, n_blk, Q_BLOCK).transpose(2, 0, 1, 3)
    k_pos = jnp.arange(S)
    scale = 1.0 / float(np.sqrt(B_HEAD_DIM))

    def attend_block(args):
        q_blk, F_blk, i = args
        q_pos = i * Q_BLOCK + jnp.arange(Q_BLOCK)
        s = jnp.einsum('bhqd,bhkd->bhqk', q_blk, k).astype(jnp.float32) * scale
        s = s + F_blk[..., None] - F[:, :, None, :]
        s = jnp.where(k_pos[None, :] <= q_pos[:, None], s, -jnp.inf)
        p = jax.nn.softmax(s, axis=-1).astype(v.dtype)
        return jnp.einsum('bhqk,bhkd->bhqd', p, v)

    o = lax.map(attend_block, (q_blocks, F_blocks, jnp.arange(n_blk)))
    o = o.transpose(1, 0, 3, 2, 4).reshape(B, S, D)
    return o @ w_out


def swiglu_ffn(h, w_gate, w_up, w_down):
    return (jax.nn.silu(h @ w_gate) * (h @ w_up)) @ w_down


def setup_inputs(seed: int = 0) -> dict:
    key = jax.random.key(seed)
    ks = jax.random.split(key, 24)
    D, F_H, AW = D_MODEL, FFN_HIDDEN, A_WIDTH
    nrm = lambda k, shape, s: jax.random.normal(k, shape, jnp.float32) * s
    x = nrm(ks[0], (BATCH, SEQ, D), 1.0)
    c = nrm(ks[1], (BATCH, D), 1.0)
    ada_w = nrm(ks[2], (DEPTH, D, N_MOD * D), 0.5 * D ** -0.5)
    ada_b = nrm(ks[3], (DEPTH, N_MOD * D), 0.02)
    norm_mix_g = 1.0 + nrm(ks[4], (DEPTH, D), 0.02)
    norm_ffn_g = 1.0 + nrm(ks[5], (DEPTH, D), 0.02)
    a_w_in = nrm(ks[6], (N_A, D, 2 * AW), D ** -0.5)
    a_b_in = nrm(ks[7], (N_A, 2 * AW), 0.02)
    a_ln_g = 1.0 + nrm(ks[8], (N_A, AW), 0.02)
    a_ln_b = nrm(ks[9], (N_A, AW), 0.02)
    a_w_s = nrm(ks[10], (N_A, A_GROUPS, CHUNK, CHUNK), CHUNK ** -0.5)
    a_b_s = 1.0 + nrm(ks[11], (N_A, A_GROUPS, CHUNK), 0.02)
    a_w_out = nrm(ks[12], (N_A, AW, D), AW ** -0.5)
    b_w_qkv = nrm(ks[13], (N_B, D, 3 * D), D ** -0.5)
    b_w_f = nrm(ks[14], (N_B, D, B_HEADS), 0.5 * D ** -0.5)
    b_w_in = jnp.concatenate([b_w_qkv, b_w_f], axis=-1)
    b_b_f = jax.random.uniform(ks[15], (N_B, B_HEADS), jnp.float32, 1.0, 6.0)
    b_w_out = nrm(ks[16], (N_B, D, D), D ** -0.5)
    ffn_w_gate = nrm(ks[17], (DEPTH, D, F_H), D ** -0.5)
    ffn_w_up = nrm(ks[18], (DEPTH, D, F_H), D ** -0.5)
    ffn_w_down = nrm(ks[19], (DEPTH, F_H, D), F_H ** -0.5)
    final_g = 1.0 + nrm(ks[20], (D,), 0.02)
    return {"x": x, "c": c, "ada_w": ada_w, "ada_b": ada_b,
            "norm_mix_g": norm_mix_g, "norm_ffn_g": norm_ffn_g,
            "a_w_in": a_w_in, "a_b_in": a_b_in, "a_ln_g": a_ln_g, "a_ln_b": a_ln_b,
            "a_w_s": a_w_s, "a_b_s": a_b_s, "a_w_out": a_w_out,
            "b_w_in": b_w_in, "b_b_f": b_b_f, "b_w_out": b_w_out,
            "ffn_w_gate": ffn_w_gate, "ffn_w_up": ffn_w_up, "ffn_w_down": ffn_w_down,
            "final_g": final_g}


def reference(x, c, ada_w, ada_b, norm_mix_g, norm_ffn_g,
              a_w_in, a_b_in, a_ln_g, a_ln_b, a_w_s, a_b_s, a_w_out,
              b_w_in, b_b_f, b_w_out,
              ffn_w_gate, ffn_w_up, ffn_w_down, final_g):
    c_act = jax.nn.silu(c)
    for i in range(DEPTH):
        mod = c_act @ ada_w[i] + ada_b[i]
        sh1, sc1, g1, sh2, sc2, g2 = jnp.split(mod, N_MOD, axis=-1)
        h = modulate(rms_norm(x, norm_mix_g[i]), sh1, sc1)
        j = i // N_MIXERS
        if i % N_MIXERS == 0:
            y = chunk_sgu_mixer(h, a_w_in[j], a_b_in[j], a_ln_g[j], a_ln_b[j],
                                a_w_s[j], a_b_s[j], a_w_out[j])
        else:
            y = forgetting_attention(h, b_w_in[j], b_b_f[j], b_w_out[j])
        x = x + g1[:, None, :] * y
        h = modulate(rms_norm(x, norm_ffn_g[i]), sh2, sc2)
        x = x + g2[:, None, :] * swiglu_ffn(h, ffn_w_gate[i], ffn_w_up[i], ffn_w_down[i])
    return rms_norm(x, final_g)
```

```python
import math
from contextlib import ExitStack

import numpy as np
import ml_dtypes

import concourse.bass as bass
import concourse.mybir as mybir
from concourse.bass_utils import run_bass_kernel_spmd

F32 = mybir.dt.float32
BF16 = mybir.dt.bfloat16
ALU = mybir.AluOpType
AF = mybir.ActivationFunctionType

D = 2048
KC = 16
FH = 5632
HC = 44
NH = 16
SEQ = 4096
NT = 4
TT = 512
TPC = 2048
EPS = 1e-6
SCALE = 1.0 / math.sqrt(128.0)
BIAS_CLAMP = 45.0
PAIRS = [[0, 1], [2, 3], [4, 5], [6, 7]]

ENGS = ("pe", "act", "dve", "pool", "sp")
EPOCH = 2000
DEPOCH = 1000


class T:
    __slots__ = ("name", "w", "r", "rd")

    def __init__(self, name=""):
        self.name = name
        self.w = None
        self.r = {}
        self.rd = []


class Op:
    __slots__ = ("eng", "fn", "deps", "inc", "dkey", "k", "unit")


class Prog:
    def __init__(self, nc, es):
        self.nc = nc
        self.es = es
        self.ops = {e: [] for e in ENGS}
        self.dma = {}
        self.last = {e: None for e in ENGS}
        self.pending = {e: None for e in ENGS}

    def barrier(self):
        lasts = [op for op in self.last.values() if op is not None]
        lasts += [ent["last"] for ent in self.dma.values() if ent["last"] is not None]
        for e in ENGS:
            self.pending[e] = list(lasts)

    def add(self, eng, fn, reads=(), writes=(), dma=None, unit=16):
        op = Op()
        op.eng = eng
        op.fn = fn
        op.inc = False
        op.dkey = None
        op.k = 0
        op.unit = unit
        raw = []
        other = []
        forced = []
        for t in reads:
            if t.w is not None:
                raw.append(t.w)
        for t in writes:
            if t.w is not None:
                other.append(t.w)
            other.extend(t.r.values())
            other.extend(t.rd)
        if dma is not None:
            ent = self.dma.setdefault(dma, {"n": 0, "last": None})
            if ent["last"] is not None:
                forced.append(ent["last"])
            op.dkey = dma
            op.k = ent["n"]
            ent["n"] += 1
            ent["last"] = op
        if self.pending[eng] is not None:
            forced.extend(self.pending[eng])
            self.pending[eng] = None
        deps = []
        seen = set()
        cands = [(d, 2) for d in forced] + [(d, 1) for d in raw] + [(d, 0) for d in other]
        for d, kind in cands:
            if id(d) in seen or d is op:
                continue
            if d.dkey is None and dma is None and d.eng == eng:
                if eng == "pe" or kind == 0:
                    continue
            seen.add(id(d))
            deps.append(d)
            if d.dkey is None:
                d.inc = True
        op.deps = deps
        for t in reads:
            if dma is not None:
                t.rd.append(op)
            else:
                t.r[eng] = op
        for t in writes:
            t.w = op
            t.r = {}
            t.rd = []
        self.ops[eng].append(op)
        if dma is None:
            self.last[eng] = op
        return op

    def emit(self):
        nc = self.nc
        es = self.es
        esem = {}
        for e in ENGS:
            k = 0
            for op in self.ops[e]:
                if op.dkey is None and op.inc:
                    op.k = k
                    k += 1
            nsem = (k + EPOCH - 1) // EPOCH
            esem[e] = [es.enter_context(nc.semaphore(f"s_{e}_{i}")) for i in range(nsem)]
        dsem = {}
        for key, ent in self.dma.items():
            nsem = (ent["n"] + DEPOCH - 1) // DEPOCH
            dsem[key] = [es.enter_context(nc.semaphore(f"d_{key}_{i}")) for i in range(nsem)]
        self.nsems = sum(len(v) for v in esem.values()) + sum(len(v) for v in dsem.values())

        def sem_of(d):
            if d.dkey is None:
                return esem[d.eng][d.k // EPOCH], d.k % EPOCH + 1, (d.eng, d.k // EPOCH)
            return (dsem[d.dkey][d.k // DEPOCH], d.unit * (d.k % DEPOCH + 1),
                    (d.dkey, d.k // DEPOCH))

        def run(e, eng):
            waited = {}
            for op in self.ops[e]:
                for d in op.deps:
                    sem, cnt, sid = sem_of(d)
                    if waited.get(sid, 0) < cnt:
                        eng.wait_ge(sem, cnt)
                        waited[sid] = cnt
                ins = op.fn(eng)
                if op.dkey is not None:
                    sem, cnt, _ = sem_of(op)
                    ins.then_inc(sem, op.unit)
                elif op.inc:
                    sem, cnt, _ = sem_of(op)
                    ins.then_inc(sem, 1)

        with nc.Block() as block:
            @block.tensor
            def _(eng):
                run("pe", eng)

            @block.scalar
            def _(eng):
                run("act", eng)

            @block.vector
            def _(eng):
                run("dve", eng)

            @block.gpsimd
            def _(eng):
                run("pool", eng)

            @block.sync
            def _(eng):
                run("sp", eng)
                for key, ent in self.dma.items():
                    if ent["last"] is not None:
                        sem, cnt, _ = sem_of(ent["last"])
                        eng.wait_ge(sem, cnt)


def gblock(r, i):
    j, e = divmod(i, 2)
    if r == 0:
        return 4 * j + (0, 3)[e]
    return 4 * j + (1, 2)[e]


def mblock(r, i):
    t, jb = divmod(i, 4)
    return t * 8 + r * 4 + jb


M_G = {}
for _r in range(2):
    for _i in range(16):
        M_G[mblock(_r, _i)] = gblock(_r, _i)


DBG = {"heads": NH, "loopB": True, "dump": False, "att": 2, "quads": 8}


def build(mode):
    nc = bass.Bass("TRN2", target_bir_lowering=False)
    doA = mode in ("A", "F")
    doB = mode in ("B", "F")

    def din(name, shape, dt=F32):
        return nc.dram_tensor(name, list(shape), dt, kind="ExternalInput").ap()

    def dout(name, shape, dt=F32):
        return nc.dram_tensor(name, list(shape), dt, kind="ExternalOutput").ap()

    def dint(name, shape, dt=F32):
        return nc.dram_tensor(name, list(shape), dt).ap()

    def dmid(name, shape, dt, produced_by_A=True):
        if mode == "F":
            return dint(name, shape, dt)
        if mode == "A":
            return dout(name, shape, dt) if produced_by_A else None
        return din(name, shape, dt)

    colv_d = din("colv", [128, 128])
    adab_d = din("adab", [128, 192])
    ident_d = din("ident", [128, 128])
    U_d = din("U", [128, 128])
    sel64_d = din("sel64", [128, 128])
    masks_d = din("masks", [128, 4, 128])
    rsel_d = din("rsel", [128, 2])
    bf_bc_d = din("bf_bc", [128, 16])
    if doA:
        x_d = din("x_sh", [TPC, D])
        cT_d = din("cT", [128, 16])
        binv_d = din("binv", [1, D])
        bs_bc_d = din("bs_bc", [128, D])
        wsT_d = din("wsT", [128, 16, 128])
        ada_w_d = din("ada_w", [2, D, 6 * D])
        a_w_in_d = din("a_w_in", [D, 2 * D])
        a_w_out_d = din("a_w_out", [D, D])
        b_w_in_d = din("b_w_in", [D, 3 * D + NH])
    b_w_out_d = din("b_w_out", [D, D]) if doB else None
    nl = 2 if mode == "F" else 1
    f_gate_d = din("f_gate", [nl, D, FH])
    f_up_d = din("f_up", [nl, D, FH])
    f_down_d = din("f_down", [nl, FH, D])

    xT_s = dmid("xT_s", [128, KC, TPC], F32)
    q_s = dmid("q_s", [128, NH, TPC], BF16)
    mod_s = dmid("mod_s", [128, 192], F32) if mode != "F" else None
    if mode == "A":
        k_loc = dout("k_loc", [NT, NH * 128, TT], BF16)
        v_loc = dout("v_loc", [NT, TT, D], BF16)
        lf_loc = dout("lf_loc", [NT, TT, NH], F32)
    elif mode == "F":
        k_loc = dint("k_loc", [NT, NH * 128, TT], BF16)
        v_loc = dint("v_loc", [NT, TT, D], BF16)
        lf_loc = dint("lf_loc", [NT, TT, NH], F32)
    if mode == "B":
        kg = din("kg", [NT, 2 * NH * 128, TT], BF16)
        vg = din("vg", [NT, 2 * TT, D], BF16)
        lfg = din("lfg", [NT, 2 * TT, NH], F32)
    elif mode == "F":
        kg = dint("kg", [NT, 2 * NH * 128, TT], BF16)
        vg = dint("vg", [NT, 2 * TT, D], BF16)
        lfg = dint("lfg", [NT, 2 * TT, NH], F32)
    if doB:
        o_s = (dout if DBG["dump"] else dint)("o_s", [128, NH, TPC], BF16)
        if DBG["dump"]:
            g_dump = dout("g_dump", [128, 32 * NH], F32)
            b_dump = dout("b_dump", [128, 8192], F32)
        out_d = dout("out_sh", [TPC, D])

    es = ExitStack()
    with es:
        def sb(name, shape, dt=F32):
            return es.enter_context(nc.sbuf_tensor("sb_" + name, list(shape), dt))

        P = Prog(nc, es)

        xT = sb("xT", [128, KC, TT], F32)
        hT = sb("hT", [128, KC, TT], BF16)
        big = sb("big", [128, 4, 8192], BF16)
        NSLOT = 3
        wsl = [sb(f"wsl{i}", [128, 8192], BF16) for i in range(NSLOT)]
        ident = sb("ident", [128, 128], F32)
        Uf = sb("Uf", [128, 128], F32)
        sel64 = sb("sel64", [128, 128], F32)
        onesf = sb("onesf", [128, 128], F32)
        onesb = sb("onesb", [128, 128], BF16)
        masks_f = sb("masks_f", [128, 4, 128], F32)
        masks_b = sb("masks_b", [128, 4, 128], BF16)
        rsel = sb("rsel", [128, 2], F32)
        bf_bc = sb("bf_bc", [128, 16], F32)
        colv = sb("colv", [128, 128], F32)
        adab = sb("adab", [128, 192], F32)
        mod = sb("mod", [128, 192], F32)
        der = sb("der", [128, 4, 16], F32)
        sq = [sb(f"sq{i}", [128, TT], BF16) for i in range(2)]
        rstd = sb("rstd", [128, TT], F32)
        tmpn = [sb(f"tmpn{i}", [128, TT], F32) for i in range(2)]
        utmp = [sb(f"utmp{i}", [128, TT], F32) for i in range(2)]
        if doA:
            cT = sb("cT", [128, 16], F32)
            cact = sb("cact", [128, 16], BF16)
            WmT = sb("WmT", [128, 16, 128], BF16)
            Qt = sb("Qt", [128, 16, 128], F32)
            binv = sb("binv", [1, D], BF16)
            wf = sb("wf", [128, KC, NH], BF16)
            lnst = sb("lnst", [128, 4, 4, 6], F32)
            lnmv = sb("lnmv", [128, 4, 2], F32)
            lnsc = sb("lnsc", [128, 4, 2], F32)
            lfst = sb("lfst", [128, 4, NH], F32)
        if doB:
            lf_all = sb("lf_all", [128, 32, NH], F32)
            G_all = sb("G_all", [128, 32, NH], F32)
            Gq = sb("Gq", [128, 16, NH], F32)
            Gq2 = sb("Gq2", [128, 16, NH], F32)
            Gref = sb("Gref", [128, 16, NH], F32)
            pts = [sb(f"pt{i}", [128, 256], BF16) for i in range(3)]
            rden = [sb(f"rden{i}", [128, 256], F32) for i in range(2)]

        banks = [es.enter_context(nc.psum_tensor(f"pb{i}", [128, 512], F32)) for i in range(8)]
        tbank = [T(f"pb{i}") for i in range(8)]
        bank_rr = [0]

        def next_bank():
            i = bank_rr[0]
            bank_rr[0] = (i + 1) % 8
            return banks[i], tbank[i]

        t_x = T("xT")
        t_h = T("hT")
        t_R = [T(f"R{i}") for i in range(4)]
        t_ws = [T(f"ws{i}") for i in range(NSLOT)]
        t_const = T("const")
        t_mod = T("mod")
        t_sq = [T("sq0"), T("sq1")]
        t_rstd = T("rstd")
        t_tmpn = [T("tmpn0"), T("tmpn1")]
        t_utmp = [T("utmp0"), T("utmp1")]
        ws_rr = [0]
        cnt = {"ev": 0}

        def load_w(src3, kc, ncols):
            i = ws_rr[0]
            ws_rr[0] = (i + 1) % NSLOT
            view = wsl[i][:, 0:kc * ncols].rearrange("p (k n) -> p k n", k=kc)
            P.add("pool", lambda e: e.dma_start(out=view, in_=src3), writes=[t_ws[i]], dma=f"w{i}")
            return view, t_ws[i]

        def wsrc(W, r0, kc, c0, ncols):
            return W[r0:r0 + kc * 128, c0:c0 + ncols].rearrange("(k p) n -> p k n", p=128)

        def evac_copy(out_ap, in_ap, reads, writes):
            cnt["ev"] += 1
            if cnt["ev"] % 2 == 0:
                P.add("act", lambda e: e.activation(out=out_ap, in_=in_ap, func=AF.Copy),
                      reads=reads, writes=writes)
            else:
                P.add("dve", lambda e: e.tensor_copy(out=out_ap, in_=in_ap), reads=reads, writes=writes)

        def ld(dst, src, t, key):
            P.add("sp", lambda e: e.dma_start(out=dst, in_=src), writes=[t], dma=key)

        ld(ident[:], ident_d, t_const, "c0")
        ld(Uf[:], U_d, t_const, "c1")
        ld(sel64[:], sel64_d, t_const, "c2")
        ld(masks_f[:], masks_d, t_const, "c3")
        ld(rsel[:], rsel_d, t_const, "c4")
        ld(bf_bc[:], bf_bc_d, t_const, "c5")
        ld(colv[:], colv_d, t_const, "c6")
        ld(adab[:], adab_d, t_const, "c7")
        P.add("dve", lambda e: e.memset(onesf[:], 1.0), writes=[t_const])
        P.add("dve", lambda e: e.memset(onesb[:], 1.0), writes=[t_const])
        P.add("dve", lambda e: e.tensor_copy(out=masks_b[:], in_=masks_f[:]), reads=[t_const], writes=[t_const])

        def cv(v):
            return colv[:, v * 16:(v + 1) * 16]

        if doA:
            t_c = T("c")
            ld(cT[:], cT_d, t_c, "c8")
            P.add("act", lambda e: e.activation(out=cact[:], in_=cT[:], func=AF.Silu), reads=[t_c], writes=[t_c])
            for li in range(2):
                pm, tpm = next_bank()
                for cc in range(24):
                    wv, tw = load_w(wsrc(ada_w_d[li], 0, KC, cc * 512, 512), KC, 512)
                    for jl in range(4):
                        j = cc * 4 + jl
                        for k in range(KC):
                            P.add("pe", lambda e, k=k, j=j, jl=jl, wv=wv, pm=pm: e.matmul(
                                pm[:, j:j + 1], lhsT=wv[:, k, jl * 128:(jl + 1) * 128], rhs=cact[:, k:k + 1],
                                start=(k == 0), stop=(k == KC - 1)), reads=[tw, t_c], writes=[tpm])
                P.add("dve", lambda e, li=li, pm=pm: e.tensor_tensor(
                    out=mod[:, li * 96:(li + 1) * 96], in0=pm[:, 0:96], in1=adab[:, li * 96:(li + 1) * 96],
                    op=ALU.add), reads=[tpm, t_const], writes=[t_mod])
            if mode == "A":
                P.add("sp", lambda e: e.dma_start(out=mod_s, in_=mod[:]), reads=[t_mod], dma="mods")
        else:
            ld(mod[:], mod_s, t_mod, "c8")

        def mv(li, which):
            o = li * 96 + which * 16
            return mod[:, o:o + 16]

        for li in range(2):
            for s, (gcol, which) in enumerate(((0 + li, 1), (2 + li, 4))):
                dst = der[:, li * 2 + s, :]
                P.add("dve", lambda e, dst=dst, li=li, which=which: e.tensor_scalar(
                    out=dst, in0=mv(li, which), scalar1=1.0, scalar2=None, op0=ALU.add),
                    reads=[t_mod], writes=[t_mod])
                P.add("dve", lambda e, dst=dst, gcol=gcol: e.tensor_tensor(
                    out=dst, in0=dst, in1=cv(gcol), op=ALU.mult), reads=[t_mod, t_const], writes=[t_mod])

        if doA:
            t_sgu = T("sguc")
            ld(Qt[:], wsT_d, t_sgu, "c9")
            P.add("dve", lambda e: e.tensor_tensor(out=Qt[:], in0=Qt[:],
                                                   in1=Uf[:].unsqueeze(1).broadcast_to([128, 16, 128]),
                                                   op=ALU.mult), reads=[t_sgu, t_const], writes=[t_sgu])
            P.add("dve", lambda e: e.tensor_copy(out=WmT[:], in_=Qt[:]), reads=[t_sgu], writes=[t_sgu])
            bsv = xT[:, 0:4, :].rearrange("p a b -> p (a b)").rearrange("p (g t) -> p g t", g=16)
            ld(xT[:, 0:4, :].rearrange("p a b -> p (a b)"), bs_bc_d, t_x, "xin")
            rb = []
            for q4 in range(4):
                pb, tpb = next_bank()
                P.add("pe", lambda e, q4=q4, pb=pb: e.matmul(
                    pb[:], lhsT=onesf[:], rhs=Qt[:, q4 * 4:(q4 + 1) * 4, :].rearrange("p a b -> p (a b)"),
                    start=True, stop=True), reads=[t_sgu, t_const], writes=[tpb])
                rb.append((pb, tpb))
            for q4 in range(4):
                pb, tpb = rb[q4]
                for gl in range(4):
                    g = q4 * 4 + gl
                    P.add("dve", lambda e, g=g, gl=gl, pb=pb: e.scalar_tensor_tensor(
                        out=Qt[:, g, :], in0=pb[:, gl * 128:(gl + 1) * 128], scalar=cv(7)[:, g:g + 1],
                        in1=bsv[:, g, :], op0=ALU.mult, op1=ALU.add),
                        reads=[tpb, t_const, t_x], writes=[t_sgu])
            t_binv = T("binv")
            P.add("pool", lambda e: e.dma_start(out=binv[:], in_=binv_d), writes=[t_binv], dma="c10")
            t_wf = T("wf")
            P.add("pool", lambda e: e.dma_start(out=wf[:], in_=wsrc(b_w_in_d, 0, KC, 3 * D, NH)),
                  writes=[t_wf], dma="c11")

        def rmsnorm_stats():
            ps, tps = next_bank()
            for dc in range(KC):
                s = dc % 2
                P.add("dve", lambda e, dc=dc, s=s: e.tensor_tensor(
                    out=sq[s][:], in0=xT[:, dc, :], in1=xT[:, dc, :], op=ALU.mult),
                    reads=[t_x], writes=[t_sq[s]])
                P.add("pe", lambda e, dc=dc, s=s, ps=ps: e.matmul(
                    ps[:], lhsT=onesb[:], rhs=sq[s][:], start=(dc == 0), stop=(dc == KC - 1)),
                    reads=[t_sq[s], t_const], writes=[tps])
            P.add("act", lambda e, ps=ps: e.activation(out=rstd[:], in_=ps[:], func=AF.Sqrt,
                                                       bias=EPS, scale=1.0 / D), reads=[tps], writes=[t_rstd])
            P.add("dve", lambda e: e.reciprocal(out=rstd[:], in_=rstd[:]), reads=[t_rstd], writes=[t_rstd])

        def norm_mod(a_ap, sh_ap):
            rmsnorm_stats()
            for dc in range(KC):
                s = dc % 2
                P.add("dve", lambda e, dc=dc, s=s: e.tensor_tensor(
                    out=tmpn[s][:], in0=xT[:, dc, :], in1=rstd[:], op=ALU.mult),
                    reads=[t_x, t_rstd], writes=[t_tmpn[s]])
                P.add("act", lambda e, dc=dc, s=s: e.activation(
                    out=hT[:, dc, :], in_=tmpn[s][:], func=AF.Identity,
                    bias=sh_ap[:, dc:dc + 1], scale=a_ap[:, dc:dc + 1]),
                    reads=[t_tmpn[s], t_mod], writes=[t_h])

        def proj_fm(W, c0, ncols_total, src, t_src, sink):
            assert ncols_total % 512 == 0
            for cc in range(ncols_total // 512):
                wv, tw = load_w(wsrc(W, 0, KC, c0 + cc * 512, 512), KC, 512)
                for jl in range(4):
                    n = cc * 4 + jl
                    ps, tps = next_bank()
                    for k in range(KC):
                        P.add("pe", lambda e, k=k, jl=jl, wv=wv, ps=ps: e.matmul(
                            ps[:], lhsT=wv[:, k, jl * 128:(jl + 1) * 128], rhs=src[:, k, :],
                            start=(k == 0), stop=(k == KC - 1)), reads=[tw, t_src], writes=[tps])
                    sink(n, ps, tps)

        def residual_sink(gate_ap):
            def sink(n, ps, tps):
                P.add("dve", lambda e, n=n, ps=ps: e.scalar_tensor_tensor(
                    out=xT[:, n, :], in0=ps[:], scalar=gate_ap[:, n:n + 1], in1=xT[:, n, :],
                    op0=ALU.mult, op1=ALU.add), reads=[tps, t_mod, t_x], writes=[t_x])
            return sink

        aT = big[:, 0:3, :].rearrange("p a b -> p (a b)")[:, 0:HC * TT].rearrange("p (j t) -> p j t", j=HC)

        def ffn(li_w, li_mod):
            norm_mod(der[:, li_mod * 2 + 1, :], mv(li_mod, 3))
            for cc in range(FH // 512):
                wg, twg = load_w(wsrc(f_gate_d[li_w], 0, KC, cc * 512, 512), KC, 512)
                wu, twu = load_w(wsrc(f_up_d[li_w], 0, KC, cc * 512, 512), KC, 512)
                for jl in range(4):
                    j = cc * 4 + jl
                    pg, tpg = next_bank()
                    pu, tpu = next_bank()
                    for k in range(KC):
                        P.add("pe", lambda e, k=k, jl=jl, wg=wg, pg=pg: e.matmul(
                            pg[:], lhsT=wg[:, k, jl * 128:(jl + 1) * 128], rhs=hT[:, k, :],
                            start=(k == 0), stop=(k == KC - 1)), reads=[twg, t_h], writes=[tpg])
                    for k in range(KC):
                        P.add("pe", lambda e, k=k, jl=jl, wu=wu, pu=pu: e.matmul(
                            pu[:], lhsT=wu[:, k, jl * 128:(jl + 1) * 128], rhs=hT[:, k, :],
                            start=(k == 0), stop=(k == KC - 1)), reads=[twu, t_h], writes=[tpu])
                    s = j % 2
                    P.add("act", lambda e, s=s, pg=pg: e.activation(out=utmp[s][:], in_=pg[:], func=AF.Silu),
                          reads=[tpg], writes=[t_utmp[s]])
                    P.add("dve", lambda e, s=s, j=j, pu=pu: e.tensor_tensor(
                        out=aT[:, j, :], in0=utmp[s][:], in1=pu[:], op=ALU.mult),
                        reads=[t_utmp[s], tpu], writes=[t_R[j // 16]])
            gate_ap = mv(li_mod, 5)
            for ng in range(4):
                pbs = [next_bank() for _ in range(4)]
                for pc in range(4):
                    wd, twd = load_w(wsrc(f_down_d[li_w], pc * 1408, 11, ng * 512, 512), 11, 512)
                    for nl_ in range(4):
                        ps, tps = pbs[nl_]
                        for jj in range(11):
                            j = pc * 11 + jj
                            P.add("pe", lambda e, jj=jj, j=j, nl_=nl_, wd=wd, ps=ps, pc=pc: e.matmul(
                                ps[:], lhsT=wd[:, jj, nl_ * 128:(nl_ + 1) * 128], rhs=aT[:, j, :],
                                start=(pc == 0 and jj == 0), stop=(pc == 3 and jj == 10)),
                                reads=[twd, t_R[j // 16]], writes=[tps])
                sink = residual_sink(gate_ap)
                for nl_ in range(4):
                    sink(ng * 4 + nl_, pbs[nl_][0], pbs[nl_][1])

        xin_v = big[:, 3, :].bitcast(F32).rearrange("p (a b) -> p a b", a=2)
        t_xin = [T("xin0"), T("xin1")]
        t_ln = T("ln")
        t_lf = T("lf")

        if doA:
            x_rows = x_d.rearrange("(n p) d -> n p d", p=128)
            k_loc_v = [k_loc[t].rearrange("(h p) n -> p h n", p=128) for t in range(NT)]
            v_loc_v = [v_loc[t].rearrange("(j p) n -> p j n", p=128) for t in range(NT)]
            lf_loc_v = [lf_loc[t].rearrange("(j p) n -> p j n", p=128) for t in range(NT)]
            v32 = big[:, 0:2, :].bitcast(F32).rearrange("p a b -> p (a b)").rearrange("p (j n) -> p j n", j=4)
            vhat = big[:, 2, :].rearrange("p (j n) -> p j n", j=4)
            yT = big[:, 3, :].rearrange("p (k n) -> p k n", k=KC)
            q_st = big[:, 0, :].rearrange("p (h n) -> p h n", h=NH)
            k_st = big[:, 1, :].rearrange("p (h n) -> p h n", h=NH)
            v_st = big[:, 2, :].rearrange("p (j n) -> p j n", j=4)

            for t in range(NT):
                for tb in range(4):
                    s = tb % 2
                    P.add("sp", lambda e, s=s, t=t, tb=tb: e.dma_start(out=xin_v[:, s, :], in_=x_rows[t * 4 + tb]),
                          writes=([t_R[3], t_xin[s]] if tb == 0 else [t_xin[s]]), dma=f"xin{s}")
                    for dg in range(4):
                        ps, tps = next_bank()
                        for dl in range(4):
                            dc = dg * 4 + dl
                            P.add("pe", lambda e, s=s, dc=dc, dl=dl, ps=ps: e.transpose(
                                ps[:, dl * 128:(dl + 1) * 128], xin_v[:, s, dc * 128:(dc + 1) * 128], ident[:]),
                                reads=[t_xin[s], t_const], writes=[tps])
                        evac_copy(xT[:, dg * 4:(dg + 1) * 4, tb * 128:(tb + 1) * 128],
                                  ps[:].rearrange("p (a b) -> p a b", a=4), [tps], [t_x])
                norm_mod(der[:, 0, :], mv(0, 0))
                for nt_ in range(4):
                    wv, tw = load_w(wsrc(a_w_in_d, 0, KC, D + nt_ * 512, 512), KC, 512)
                    for tb in range(4):
                        ps, tps = next_bank()
                        for k in range(KC):
                            P.add("pe", lambda e, k=k, tb=tb, wv=wv, ps=ps: e.matmul(
                                ps[:], lhsT=hT[:, k, tb * 128:(tb + 1) * 128], rhs=wv[:, k, :],
                                start=(k == 0), stop=False), reads=[tw, t_h], writes=[tps])
                        P.add("pe", lambda e, nt_=nt_, ps=ps: e.matmul(
                            ps[:], lhsT=onesb[0:1, :], rhs=binv[0:1, nt_ * 512:(nt_ + 1) * 512],
                            start=False, stop=True), reads=[t_binv, t_const], writes=[tps])
                        P.add("act", lambda e, tb=tb, nt_=nt_, ps=ps: e.activation(
                            out=v32[:, tb, nt_ * 512:(nt_ + 1) * 512], in_=ps[:], func=AF.Gelu),
                            reads=[tps], writes=[t_R[0], t_R[1]])
                for tb in range(4):
                    for c4 in range(4):
                        P.add("dve", lambda e, tb=tb, c4=c4: e.bn_stats(
                            out=lnst[:, tb, c4, :], in_=v32[:, tb, c4 * 512:(c4 + 1) * 512]),
                            reads=[t_R[0], t_R[1]], writes=[t_ln])
                    P.add("dve", lambda e, tb=tb: e.bn_aggr(out=lnmv[:, tb, :], in_=lnst[:, tb, :, :]),
                          reads=[t_ln], writes=[t_ln])
                P.add("act", lambda e: e.activation(out=lnsc[:, :, 0:1], in_=lnmv[:, :, 1:2], func=AF.Sqrt,
                                                    bias=EPS, scale=1.0), reads=[t_ln], writes=[t_ln])
                P.add("dve", lambda e: e.reciprocal(out=lnsc[:, :, 0:1], in_=lnsc[:, :, 0:1]),
                      reads=[t_ln], writes=[t_ln])
                P.add("dve", lambda e: e.scalar_tensor_tensor(
                    out=lnsc[:, :, 1:2], in0=lnmv[:, :, 0:1], scalar=-1.0, in1=lnsc[:, :, 0:1],
                    op0=ALU.mult, op1=ALU.mult), reads=[t_ln], writes=[t_ln])
                for tb in range(4):
                    P.add("act", lambda e, tb=tb: e.activation(
                        out=vhat[:, tb, :], in_=v32[:, tb, :], func=AF.Identity,
                        bias=lnsc[:, tb, 1:2], scale=lnsc[:, tb, 0:1]),
                        reads=[t_ln, t_R[0], t_R[1]], writes=[t_R[2]])

                def sgu_sink(g, ps, tps):
                    s = g % 2
                    P.add("act", lambda e, s=s, g=g, ps=ps: e.activation(
                        out=utmp[s][:], in_=ps[:], func=AF.Gelu, bias=cv(5)[:, g:g + 1], scale=1.0),
                        reads=[tps, t_const], writes=[t_utmp[s]])
                    p2, tp2 = next_bank()
                    for tb in range(4):
                        P.add("pe", lambda e, tb=tb, g=g, p2=p2: e.matmul(
                            p2[:, tb * 128:(tb + 1) * 128], lhsT=vhat[:, tb, g * 128:(g + 1) * 128],
                            rhs=WmT[:, g, :], start=True, stop=True),
                            reads=[t_R[2], t_sgu], writes=[tp2])
                    s2 = g % 2
                    P.add("dve", lambda e, g=g, p2=p2, s2=s2: e.scalar_tensor_tensor(
                        out=tmpn[s2][:].rearrange("p (a b) -> p a b", a=4),
                        in0=p2[:].rearrange("p (a b) -> p a b", a=4), scalar=cv(6)[:, g:g + 1],
                        in1=Qt[:, g, :].unsqueeze(1).broadcast_to([128, 4, 128]),
                        op0=ALU.mult, op1=ALU.add),
                        reads=[tp2, t_const, t_sgu], writes=[t_tmpn[s2]])
                    P.add("dve", lambda e, g=g, s=s, s2=s2: e.tensor_tensor(
                        out=yT[:, g, :], in0=tmpn[s2][:], in1=utmp[s][:], op=ALU.mult),
                        reads=[t_tmpn[s2], t_utmp[s]], writes=[t_R[3]])
                proj_fm(a_w_in_d, 0, D, hT, t_h, sgu_sink)
                proj_fm(a_w_out_d, 0, D, yT, t_R[3], residual_sink(mv(0, 2)))
                ffn(0, 0)
                norm_mod(der[:, 2, :], mv(1, 0))

                def q_sink(n, ps, tps):
                    evac_copy(q_st[:, n, :], ps[:], [tps], [t_R[0]])

                def k_sink(n, ps, tps):
                    evac_copy(k_st[:, n, :], ps[:], [tps], [t_R[1]])
                proj_fm(b_w_in_d, 0, D, hT, t_h, q_sink)
                P.add("sp", lambda e, t=t: e.dma_start(out=q_s[:, :, t * TT:(t + 1) * TT], in_=q_st),
                      reads=[t_R[0]], dma="qst")
                proj_fm(b_w_in_d, D, D, hT, t_h, k_sink)
                t_kl = T("kl")
                P.add("sp", lambda e, t=t: e.dma_start(out=k_loc_v[t], in_=k_st), reads=[t_R[1]],
                      writes=[t_kl], dma="kst")
                for nt_ in range(4):
                    wv, tw = load_w(wsrc(b_w_in_d, 0, KC, 2 * D + nt_ * 512, 512), KC, 512)
                    for tb in range(4):
                        ps, tps = next_bank()
                        for k in range(KC):
                            P.add("pe", lambda e, k=k, tb=tb, wv=wv, ps=ps: e.matmul(
                                ps[:], lhsT=hT[:, k, tb * 128:(tb + 1) * 128], rhs=wv[:, k, :],
                                start=(k == 0), stop=(k == KC - 1)), reads=[tw, t_h], writes=[tps])
                        evac_copy(v_st[:, tb, nt_ * 512:(nt_ + 1) * 512], ps[:], [tps], [t_R[2]])
                t_vl = T("vl")
                P.add("sp", lambda e, t=t: e.dma_start(out=v_loc_v[t], in_=v_st), reads=[t_R[2]],
                      writes=[t_vl], dma="vst")
                pf, tpf = next_bank()
                for tb in range(4):
                    for k in range(KC):
                        P.add("pe", lambda e, k=k, tb=tb, pf=pf: e.matmul(
                            pf[:, tb * NH:(tb + 1) * NH], lhsT=hT[:, k, tb * 128:(tb + 1) * 128], rhs=wf[:, k, :],
                            start=(k == 0), stop=(k == KC - 1)), reads=[t_wf, t_h], writes=[tpf])
                P.add("dve", lambda e, pf=pf: e.tensor_tensor(
                    out=lfst[:], in0=pf[:, 0:4 * NH].rearrange("p (a b) -> p a b", a=4),
                    in1=bf_bc[:].unsqueeze(1).broadcast_to([128, 4, NH]), op=ALU.add),
                    reads=[tpf, t_const], writes=[t_lf])
                P.add("act", lambda e: e.activation(out=lfst[:], in_=lfst[:], func=AF.Exp, scale=-1.0),
                      reads=[t_lf], writes=[t_lf])
                P.add("act", lambda e: e.activation(out=lfst[:], in_=lfst[:], func=AF.Ln, bias=1.0, scale=1.0),
                      reads=[t_lf], writes=[t_lf])
                t_ll = T("ll")
                P.add("sp", lambda e, t=t: e.dma_start(out=lf_loc_v[t], in_=lfst[:]), reads=[t_lf],
                      writes=[t_ll], dma="lfst")
                P.add("sp", lambda e, t=t: e.dma_start(out=xT_s[:, :, t * TT:(t + 1) * TT], in_=xT[:]),
                      reads=[t_x], dma="xsp")
                if mode == "F":
                    P.add("pool", lambda e, t=t: e.collective_compute(
                        "AllGather", ALU.bypass, replica_groups=PAIRS, ins=[k_loc[t].opt()], outs=[kg[t].opt()]),
                        reads=[t_kl], dma="cck", unit=1)
                    P.add("pool", lambda e, t=t: e.collective_compute(
                        "AllGather", ALU.bypass, replica_groups=PAIRS, ins=[v_loc[t].opt()], outs=[vg[t].opt()]),
                        reads=[t_vl], dma="ccv", unit=1)
                    P.add("pool", lambda e, t=t: e.collective_compute(
                        "AllGather", ALU.bypass, replica_groups=PAIRS, ins=[lf_loc[t].opt()], outs=[lfg[t].opt()]),
                        reads=[t_ll], dma="ccl", unit=1)

        if doB:
            P.barrier()
            t_g = T("G")
            lfg_v = lfg.rearrange("t (r j p) h -> p t r j h", r=2, j=4, p=128)
            for t in range(NT):
                P.add("sp", lambda e, t=t: e.dma_start(
                    out=lf_all[:, t * 8:(t + 1) * 8, :].rearrange("p (r j) h -> p r j h", r=2),
                    in_=lfg_v[:, t]), writes=[t_g], dma="lfl")
            pG, tpG = next_bank()
            order = sorted(range(32), key=lambda m: M_G[m])
            for mt in range(32):
                preds = [ms for ms in order if M_G[ms] < M_G[mt]]
                n = len(preds)
                for idx, ms in enumerate(preds):
                    P.add("pe", lambda e, mt=mt, ms=ms, idx=idx: e.matmul(
                        pG[:, mt * NH:(mt + 1) * NH], lhsT=onesf[:], rhs=lf_all[:, ms, :],
                        start=(idx == 0), stop=False), reads=[t_g, t_const], writes=[tpG])
                P.add("pe", lambda e, mt=mt, n=n: e.matmul(
                    pG[:, mt * NH:(mt + 1) * NH], lhsT=Uf[:], rhs=lf_all[:, mt, :],
                    start=(n == 0), stop=True), reads=[t_g, t_const], writes=[tpG])
            t_G2 = T("G2")
            P.add("dve", lambda e: e.tensor_copy(out=G_all[:].rearrange("p m h -> p (m h)"), in_=pG[:]),
                  reads=[tpG], writes=[t_G2])
            Gv = G_all[:].rearrange("p (t r j) h -> p t r j h", t=4, r=2)
            P.add("dve", lambda e: e.tensor_scalar(
                out=Gq[:].rearrange("p (t j) h -> p t j h", t=4), in0=Gv[:, :, 0], scalar1=rsel[:, 0:1],
                scalar2=None, op0=ALU.mult), reads=[t_G2, t_const], writes=[t_G2])
            P.add("dve", lambda e: e.scalar_tensor_tensor(
                out=Gq2[:].rearrange("p (t j) h -> p t j h", t=4), in0=Gv[:, :, 1], scalar=rsel[:, 1:2],
                in1=Gq[:].rearrange("p (t j) h -> p t j h", t=4), op0=ALU.mult, op1=ALU.add),
                reads=[t_G2, t_const], writes=[t_G2])
            pR, tpR = next_bank()
            P.add("pe", lambda e: e.matmul(pR[:, 0:256], lhsT=sel64[:], rhs=Gq2[:].rearrange("p i h -> p (i h)"),
                                           start=True, stop=True), reads=[t_G2, t_const], writes=[tpR])
            P.add("dve", lambda e: e.tensor_copy(out=Gref[:].rearrange("p i h -> p (i h)"), in_=pR[:, 0:256]),
                  reads=[tpR], writes=[t_G2])
            biasT = xT[:].rearrange("p a b -> p (a b)").rearrange("p (h i m) -> p h i m", h=NH, i=16)
            t_bias = T("bias")
            for i in range(16):
                P.add("dve", lambda e, i=i: e.tensor_tensor(
                    out=biasT[:, :, i, :], in0=G_all[:].rearrange("p m h -> p h m"),
                    in1=Gref[:, i, :].unsqueeze(2).broadcast_to([128, NH, 32]), op=ALU.subtract),
                    reads=[t_G2], writes=[t_bias])
                P.add("dve", lambda e, i=i: e.tensor_scalar(
                    out=biasT[:, :, i, :], in0=biasT[:, :, i, :], scalar1=BIAS_CLAMP, scalar2=None, op0=ALU.min),
                    reads=[t_bias], writes=[t_bias])

            if DBG["dump"]:
                P.add("sp", lambda e: e.dma_start(out=g_dump, in_=G_all[:].rearrange("p m h -> p (m h)")),
                      reads=[t_G2], dma="gdump")
                P.add("sp", lambda e: e.dma_start(out=b_dump, in_=xT[:].rearrange("p a b -> p (a b)")),
                      reads=[t_bias], dma="bdump")
            Ksl = [hT[:].rearrange("p a b -> p (a b)")[:, s * 4096:(s + 1) * 4096] for s in range(2)]
            t_K = [T("K0"), T("K1")]
            V4 = [big[:, 2 * s:2 * s + 2, :].rearrange("p a b -> p (a b)").rearrange("p (m c) -> p m c", m=32)
                  for s in range(2)]
            t_V = [T("V0"), T("V1")]
            qsl = [wsl[0][:, s * 2048:(s + 1) * 2048] for s in range(2)]
            t_q = [T("q0"), T("q1")]
            osl = [wsl[0][:, 4096 + s * 2048:4096 + (s + 1) * 2048] for s in range(2)]
            t_o = [T("o0"), T("o1")]
            t_pt = [T("pt0"), T("pt1"), T("pt2")]
            t_rden = [T("rd0"), T("rd1")]
            kg_v = kg.rearrange("t (r h p) n -> p h t r n", r=2, h=NH, p=128)
            vg_v = vg.rearrange("t (r j p) c -> p t r j c", r=2, j=4, p=128)
            att = {"s": 0, "pt": 0, "q": 0}

            for h in range(DBG["heads"]):
                ks = h % 2
                for t in range(NT):
                    P.add("sp", lambda e, h=h, t=t, ks=ks: e.dma_start(
                        out=Ksl[ks][:, t * 1024:(t + 1) * 1024].rearrange("p (r n) -> p r n", r=2),
                        in_=kg_v[:, h, t]), writes=[t_K[ks]], dma=f"kl{ks}_{t}")
                P.add("sp", lambda e, h=h, ks=ks: e.dma_start(out=qsl[ks], in_=q_s[:, h, :]),
                      writes=[t_q[ks]], dma=f"ql{ks}")
                if h % 4 == 0:
                    vs = (h // 4) % 2
                    for t in range(NT):
                        for r in range(2):
                            P.add("sp", lambda e, h=h, t=t, r=r, vs=vs: e.dma_start(
                                out=V4[vs][:, t * 8 + r * 4:t * 8 + r * 4 + 4, :],
                                in_=vg_v[:, t, r, :, h * 128:h * 128 + 512]),
                                writes=[t_V[vs]], dma=f"vl{vs}_{t}{r}")
                vs = (h // 4) % 2
                hv = (h % 4) * 128
                for j in range(DBG["quads"]):
                    qa = att["q"] % 2
                    att["q"] += 1
                    po, tpo = banks[qa], tbank[qa]
                    pd, tpd = banks[2 + qa], tbank[2 + qa]
                    tq, jb = divmod(2 * j, 4)
                    below = [mblock(r_, i_) for i_ in range(2 * j) for r_ in range(2)]
                    mA = tq * 8 + jb
                    mB = tq * 8 + 4 + jb
                    mC = tq * 8 + 4 + jb + 1
                    mD = tq * 8 + jb + 1
                    blocks = [(m, 0, 256, None) for m in below]
                    blocks += [(mA, 0, 256, 0), (mB, 0, 256, 1), (mC, 128, 128, 2), (mD, 128, 128, 3)]
                    nb = len(blocks)
                    qc0 = 2 * j * 128
                    pend = []

                    def issue_S(bi, ks=ks, h=h, j=j, qc0=qc0, blocks=blocks):
                        m, c0, n, mk = blocks[bi]
                        sbk = 4 + att["s"] % 4
                        att["s"] += 1
                        ps_, tps_ = banks[sbk], tbank[sbk]
                        P.add("pe", lambda e, m=m, c0=c0, n=n, ps_=ps_, ks=ks, qc0=qc0: e.matmul(
                            ps_[:, c0:c0 + n], lhsT=Ksl[ks][:, m * 128:(m + 1) * 128],
                            rhs=qsl[ks][:, qc0 + c0:qc0 + c0 + n], start=True, stop=True),
                            reads=[t_K[ks], t_q[ks]], writes=[tps_])
                        pi = att["pt"] % 3
                        att["pt"] += 1
                        for e_ in range(c0 // 128, (c0 + n) // 128):
                            P.add("act", lambda e, m=m, e_=e_, ps_=ps_, pi=pi, h=h, j=j: e.activation(
                                out=pts[pi][:, e_ * 128:(e_ + 1) * 128], in_=ps_[:, e_ * 128:(e_ + 1) * 128],
                                func=AF.Exp, bias=biasT[:, h, 2 * j + e_, m:m + 1], scale=SCALE),
                                reads=[tps_, t_bias], writes=[t_pt[pi]])
                        if mk is not None:
                            e_ = 0 if mk < 2 else 1
                            P.add("dve", lambda e, mk=mk, e_=e_, pi=pi: e.tensor_tensor(
                                out=pts[pi][:, e_ * 128:(e_ + 1) * 128], in0=pts[pi][:, e_ * 128:(e_ + 1) * 128],
                                in1=masks_b[:, mk, :], op=ALU.mult), reads=[t_pt[pi], t_const], writes=[t_pt[pi]])
                        return pi

                    def issue_PV(bi, pi, blocks=blocks, nb=nb, po=po, pd=pd, tpo=tpo, tpd=tpd, vs=vs, hv=hv):
                        m, c0, n, mk = blocks[bi]
                        if DBG["att"] < 2:
                            return
                        first = (bi == 0)
                        last = (bi == nb - 1)
                        P.add("pe", lambda e, m=m, c0=c0, n=n, pi=pi, po=po, vs=vs, hv=hv, first=first, last=last: e.matmul(
                            po[:, c0:c0 + n], lhsT=V4[vs][:, m, hv:hv + 128], rhs=pts[pi][:, c0:c0 + n],
                            start=first, stop=last, skip_group_check=True),
                            reads=[t_V[vs], t_pt[pi]], writes=[tpo])
                        P.add("pe", lambda e, c0=c0, n=n, pi=pi, pd=pd, first=first, last=last: e.matmul(
                            pd[:, c0:c0 + n], lhsT=onesb[:], rhs=pts[pi][:, c0:c0 + n],
                            start=first, stop=last, skip_group_check=True),
                            reads=[t_const, t_pt[pi]], writes=[tpd])

                    LOOK = 2
                    if DBG["att"] < 1:
                        nb = 0
                    for bi in range(min(LOOK, nb)):
                        pend.append(issue_S(bi))
                    for bi in range(nb):
                        issue_PV(bi, pend[bi])
                        if bi + LOOK < nb:
                            pend.append(issue_S(bi + LOOK))
                    ri = att["q"] % 2
                    if DBG["att"] < 2:
                        continue
                    P.add("dve", lambda e, ri=ri, pd=pd: e.reciprocal(out=rden[ri][:], in_=pd[:, 0:256]),
                          reads=[tpd], writes=[t_rden[ri]])
                    P.add("dve", lambda e, ri=ri, po=po, j=j, ks=ks: e.tensor_tensor(
                        out=osl[ks][:, j * 256:(j + 1) * 256], in0=po[:, 0:256], in1=rden[ri][:], op=ALU.mult),
                        reads=[tpo, t_rden[ri]], writes=[t_o[ks]])
                P.add("sp", lambda e, h=h, ks=ks: e.dma_start(out=o_s[:, h, :], in_=osl[ks]),
                      reads=[t_o[ks]], dma=f"os{ks}")

            P.barrier()
            out_rows = out_d.rearrange("(n p) d -> n p d", p=128)
            liw = 1 if mode == "F" else 0
            for t in range(NT if DBG["loopB"] else 0):
                P.add("sp", lambda e, t=t: e.dma_start(out=xT[:], in_=xT_s[:, :, t * TT:(t + 1) * TT]),
                      writes=[t_x], dma="xld")
                P.add("sp", lambda e, t=t: e.dma_start(out=hT[:], in_=o_s[:, :, t * TT:(t + 1) * TT]),
                      writes=[t_h], dma="old")
                proj_fm(b_w_out_d, 0, D, hT, t_h, residual_sink(mv(1, 2)))
                ffn(liw, 1)
                rmsnorm_stats()
                for dc in range(KC):
                    P.add("dve", lambda e, dc=dc: e.scalar_tensor_tensor(
                        out=xT[:, dc, :], in0=xT[:, dc, :], scalar=cv(4)[:, dc:dc + 1], in1=rstd[:],
                        op0=ALU.mult, op1=ALU.mult), reads=[t_x, t_rstd, t_const], writes=[t_x])
                for tb in range(4):
                    s = tb % 2
                    for dg in range(4):
                        ps, tps = next_bank()
                        for dl in range(4):
                            dc = dg * 4 + dl
                            P.add("pe", lambda e, dc=dc, dl=dl, tb=tb, ps=ps: e.transpose(
                                ps[:, dl * 128:(dl + 1) * 128], xT[:, dc, tb * 128:(tb + 1) * 128], ident[:]),
                                reads=[t_x, t_const], writes=[tps])
                        evac_copy(xin_v[:, s, dg * 512:(dg + 1) * 512], ps[:], [tps], [t_xin[s]])
                    P.add("sp", lambda e, s=s, t=t, tb=tb: e.dma_start(out=out_rows[t * 4 + tb], in_=xin_v[:, s, :]),
                          reads=[t_xin[s]], dma=f"ost{s}")
        P.emit()
        build.nsems = P.nsems
        build.nops = {e: len(P.ops[e]) for e in ENGS}
    return nc


def _col(v):
    return np.ascontiguousarray(np.asarray(v, np.float32).reshape(-1, 128).T)


def _host_inputs(inp):
    x = np.asarray(inp["x"], np.float32)
    c = np.asarray(inp["c"], np.float32)
    ada_b = np.asarray(inp["ada_b"], np.float32)
    colv = np.concatenate([
        _col(inp["norm_mix_g"][0]), _col(inp["norm_mix_g"][1]),
        _col(inp["norm_ffn_g"][0]), _col(inp["norm_ffn_g"][1]),
        _col(inp["final_g"]), _col(np.asarray(inp["a_b_in"])[0, :D]),
        _col(inp["a_ln_g"][0]), _col(inp["a_ln_b"][0])], axis=1)
    adab = np.concatenate([_col(ada_b[0]), _col(ada_b[1])], axis=1)
    U = np.triu(np.ones((128, 128), np.float32))
    sel64 = np.zeros((128, 128), np.float32)
    sel64[64, :] = 1.0
    ones = np.ones((128, 128), np.float32)
    zeros = np.zeros((128, 128), np.float32)
    masks = {0: np.stack([U, zeros, ones, U], axis=1), 1: np.stack([ones, U, U, zeros], axis=1)}
    shared = {
        "colv": np.ascontiguousarray(colv), "adab": np.ascontiguousarray(adab),
        "ident": np.eye(128, dtype=np.float32), "U": U, "sel64": sel64,
        "bf_bc": np.ascontiguousarray(np.broadcast_to(np.asarray(inp["b_b_f"], np.float32)[0][None, :], (128, NH))),
        "binv": np.ascontiguousarray(np.asarray(inp["a_b_in"], np.float32)[0, D:][None, :]),
        "bs_bc": np.ascontiguousarray(np.broadcast_to(
            np.asarray(inp["a_b_s"], np.float32)[0].reshape(1, D), (128, D))),
        "wsT": np.ascontiguousarray(np.asarray(inp["a_w_s"], np.float32)[0].transpose(2, 0, 1)),
        "ada_w": np.asarray(inp["ada_w"], np.float32),
        "a_w_in": np.asarray(inp["a_w_in"], np.float32)[0],
        "a_w_out": np.asarray(inp["a_w_out"], np.float32)[0],
        "b_w_in": np.asarray(inp["b_w_in"], np.float32)[0],
        "b_w_out": np.asarray(inp["b_w_out"], np.float32)[0],
    }
    per_core = []
    for cidx in range(8):
        b, r = divmod(cidx, 2)
        rows = np.concatenate([x[b, gblock(r, i) * 128:(gblock(r, i) + 1) * 128, :] for i in range(16)], axis=0)
        per_core.append({
            "x_sh": np.ascontiguousarray(rows),
            "cT": _col(c[b]),
            "masks": np.ascontiguousarray(masks[r]),
            "rsel": np.ascontiguousarray(np.broadcast_to(np.array([1.0 - r, float(r)], np.float32)[None, :], (128, 2))),
        })
    return shared, per_core


def _assemble(outs):
    res = np.empty((4, SEQ, D), np.float32)
    for cidx in range(8):
        b, r = divmod(cidx, 2)
        o = outs[cidx]
        for i in range(16):
            g = gblock(r, i)
            res[b, g * 128:(g + 1) * 128, :] = o[i * 128:(i + 1) * 128, :]
    return res


_NC_CACHE = {}


def _get_nc(mode):
    if mode not in _NC_CACHE:
        _NC_CACHE[mode] = build(mode)
    return _NC_CACHE[mode]


FUSED = False


def kernel(**inp):
    shared, per_core = _host_inputs(inp)
    fg = np.asarray(inp["ffn_w_gate"], np.float32)
    fu = np.asarray(inp["ffn_w_up"], np.float32)
    fd = np.asarray(inp["ffn_w_down"], np.float32)
    cores = list(range(8))
    if FUSED:
        keysF = ["colv", "adab", "ident", "U", "sel64", "bf_bc", "binv", "bs_bc", "wsT", "ada_w",
                 "a_w_in", "a_w_out", "b_w_in", "b_w_out"]
        maps = []
        for cidx in cores:
            m = {k: shared[k] for k in keysF}
            m.update(per_core[cidx])
            m.update({"f_gate": fg, "f_up": fu, "f_down": fd})
            maps.append(m)
        res = run_bass_kernel_spmd(_get_nc("F"), maps, core_ids=cores)
        return _assemble([r["out_sh"] for r in res.results])
    keysA = ["colv", "adab", "ident", "U", "sel64", "bf_bc", "binv", "bs_bc", "wsT", "ada_w",
             "a_w_in", "a_w_out", "b_w_in"]
    mapsA = []
    for cidx in cores:
        m = {k: shared[k] for k in keysA}
        m.update(per_core[cidx])
        m.update({"f_gate": fg[0:1], "f_up": fu[0:1], "f_down": fd[0:1]})
        mapsA.append(m)
    ra = run_bass_kernel_spmd(_get_nc("A"), mapsA, core_ids=cores).results
    keysB = ["colv", "adab", "ident", "U", "sel64", "bf_bc", "b_w_out"]
    mapsB = []
    for cidx in cores:
        b = cidx // 2
        m = {k: shared[k] for k in keysB}
        m["masks"] = per_core[cidx]["masks"]
        m["rsel"] = per_core[cidx]["rsel"]
        m.update({"f_gate": fg[1:2], "f_up": fu[1:2], "f_down": fd[1:2]})
        m["xT_s"] = ra[cidx]["xT_s"]
        m["q_s"] = ra[cidx]["q_s"]
        m["mod_s"] = ra[cidx]["mod_s"]
        for nm, src in (("kg", "k_loc"), ("vg", "v_loc"), ("lfg", "lf_loc")):
            a0 = np.asarray(ra[2 * b][src])
            a1 = np.asarray(ra[2 * b + 1][src])
            m[nm] = np.ascontiguousarray(np.concatenate([a0, a1], axis=1))
        mapsB.append(m)
    rb = run_bass_kernel_spmd(_get_nc("B"), mapsB, core_ids=cores).results
    return _assemble([r["out_sh"] for r in rb])
```
